# Optimizing a Trainium2 kernel written in Bass

```python
import math
import jax
import jax.numpy as jnp
from jax import lax
import numpy as np

D_MODEL = 2048
BATCH = 4
SEQ = 2048
DEPTH = 1

NORM_EPS = 1e-5
DILATED_CONFIGS = ((128, 1), (512, 4), (2048, 16))
N_ATTN_GROUPS = 3
ATTN_HEADS_PER_GROUP = 8
ATTN_HEAD_DIM = 128
ATTN_WIDTH = N_ATTN_GROUPS * ATTN_HEADS_PER_GROUP * ATTN_HEAD_DIM
ATTN_OUT_WIDTH = ATTN_HEADS_PER_GROUP * ATTN_HEAD_DIM
ATTN_BLOCK = 128
ROPE_DIMS = ATTN_HEAD_DIM // 4
ROPE_THETA = 500000.0
SSD_INNER = 2 * D_MODEL
SSD_HEAD_DIM = 64
SSD_HEADS = SSD_INNER // SSD_HEAD_DIM
SSD_GROUPS = 8
SSD_HEADS_PER_GROUP = SSD_HEADS // SSD_GROUPS
SSD_STATE = 128
SSD_CONV = 4
SSD_CHUNK = 128
SSD_CONV_CH = SSD_INNER + 2 * SSD_GROUPS * SSD_STATE
N_BRANCHES = 2
SPLIT_SIZES = (ATTN_WIDTH, ATTN_WIDTH, ATTN_WIDTH, ATTN_OUT_WIDTH, SSD_INNER, SSD_CONV_CH, SSD_HEADS, N_BRANCHES * D_MODEL)
D_IN_PROJ = sum(SPLIT_SIZES)

kernel_name = "hybrid_dilated_attn_ssd_gated_block"


def rms_norm(x, w):
    xf = x.astype(jnp.float32)
    xf = xf * lax.rsqrt(jnp.mean(xf * xf, axis=-1, keepdims=True) + NORM_EPS)
    return (xf * w.astype(jnp.float32)).astype(x.dtype)


def partial_rope(t, positions):
    half = ROPE_DIMS // 2
    inv_freq = ROPE_THETA ** (-jnp.arange(half, dtype=jnp.float32) / half)
    ang = positions.astype(jnp.float32)[..., None] * inv_freq
    cos = jnp.cos(ang)[:, :, None, None, :]
    sin = jnp.sin(ang)[:, :, None, None, :]
    tf = t.astype(jnp.float32)
    x1 = tf[..., :half]
    x2 = tf[..., half:ROPE_DIMS]
    out = jnp.concatenate([x1 * cos - x2 * sin, x2 * cos + x1 * sin, tf[..., ROPE_DIMS:]], axis=-1)
    return out.astype(t.dtype)


def dilated_group_attention(q, k, v, window, dilation):
    b, s, h, e = q.shape
    n_sub = window // dilation
    blk = ATTN_BLOCK
    span = dilation * blk
    sp = -(-s // span) * span
    nb = sp // span

    def to_blocks(t):
        t = jnp.pad(t, ((0, 0), (0, sp - s), (0, 0), (0, 0)))
        t = t.reshape(b, nb * blk, dilation, h, e).transpose(0, 2, 1, 3, 4)
        return t.reshape(b, dilation, nb, blk, h, e)

    def with_prev_block(t):
        prev = jnp.pad(t[:, :, :-1], ((0, 0), (0, 0), (1, 0), (0, 0), (0, 0), (0, 0)))
        return jnp.concatenate([prev, t], axis=3)

    qb = to_blocks(q)
    kc = with_prev_block(to_blocks(k))
    vc = with_prev_block(to_blocks(v))
    scale = ATTN_HEAD_DIM ** -0.5
    scores = jnp.einsum("brnqhe,brnkhe->brnhqk", qb, kc).astype(jnp.float32) * scale
    qi = jnp.arange(blk)[:, None]
    ki = jnp.arange(2 * blk)[None, :]
    dist = qi + blk - ki
    band = (dist >= 0) & (dist <= n_sub)
    has_prev = (jnp.arange(nb)[:, None, None] > 0) | (ki >= blk)[None]
    valid = band[None] & has_prev
    scores = jnp.where(valid[None, None, :, None], scores, -jnp.inf)
    m = jnp.max(scores, axis=-1, keepdims=True)
    p = jnp.exp(scores - m)
    den = jnp.sum(p, axis=-1)
    o = jnp.einsum("brnhqk,brnkhe->brnqhe", p, vc.astype(jnp.float32)) / jnp.swapaxes(den, -1, -2)[..., None]
    lse = jnp.swapaxes(m[..., 0] + jnp.log(den), -1, -2)

    def from_blocks(t):
        tail = t.shape[5:]
        t = t.reshape((b, dilation, nb * blk, h) + tail)
        t = jnp.moveaxis(t, 1, 2).reshape((b, sp, h) + tail)
        return t[:, :s]

    return from_blocks(o), from_blocks(lse)


def dilated_mixture_attention(q, k, v):
    b, s = q.shape[:2]
    outs, lses = [], []
    for g, (window, dilation) in enumerate(DILATED_CONFIGS):
        o, l = dilated_group_attention(q[:, :, g], k[:, :, g], v[:, :, g], window, dilation)
        outs.append(o)
        lses.append(l)
    w = jax.nn.softmax(jnp.stack(lses, axis=0), axis=0)
    out = jnp.sum(w[..., None] * jnp.stack(outs, axis=0), axis=0)
    return out.reshape(b, s, ATTN_OUT_WIDTH)


def causal_depthwise_conv(x, w, bias):
    y = lax.conv_general_dilated(
        x, w[:, None, :].astype(x.dtype), window_strides=(1,), padding=[(SSD_CONV - 1, 0)],
        dimension_numbers=("NWC", "WIO", "NWC"), feature_group_count=x.shape[-1])
    return y + bias.astype(x.dtype)


def segsum_exp(a):
    cs = jnp.cumsum(a, axis=-1)
    diff = cs[..., :, None] - cs[..., None, :]
    l = a.shape[-1]
    mask = jnp.tril(jnp.ones((l, l), dtype=bool))
    return jnp.exp(jnp.where(mask, diff, -jnp.inf))


def ssd_mixer(xbc, z, dt_raw, conv_w, conv_b, dt_bias, a_log, d_skip, norm_w):
    b, s, _ = xbc.shape
    G, J, P, N, L = SSD_GROUPS, SSD_HEADS_PER_GROUP, SSD_HEAD_DIM, SSD_STATE, SSD_CHUNK
    c = s // L
    xbc = jax.nn.silu(causal_depthwise_conv(xbc, conv_w, conv_b)).astype(jnp.float32)
    xs, bm, cm = jnp.split(xbc, [SSD_INNER, SSD_INNER + G * N], axis=-1)
    xs = xs.reshape(b, c, L, G, J, P)
    bm = bm.reshape(b, c, L, G, N)
    cm = cm.reshape(b, c, L, G, N)
    dt = jax.nn.softplus(dt_raw.astype(jnp.float32) + dt_bias.astype(jnp.float32)).reshape(b, c, L, G, J)
    a = -jnp.exp(a_log.astype(jnp.float32)).reshape(G, J)
    xdt = xs * dt[..., None]
    da = jnp.transpose(dt * a, (0, 1, 3, 4, 2))
    cs = jnp.cumsum(da, axis=-1)
    decay_in = segsum_exp(da)
    cb = jnp.einsum("bclgn,bcsgn->bcgls", cm, bm)
    y_diag = jnp.einsum("bcgls,bcgjls,bcsgjp->bclgjp", cb, decay_in, xdt)
    decay_to_end = jnp.exp(cs[..., -1:] - cs)
    chunk_states = jnp.einsum("bclgn,bcgjl,bclgjp->bcgjpn", bm, decay_to_end, xdt)
    chunk_decay = jnp.exp(cs[..., -1])

    def step(state, inp):
        decay, st = inp
        return state * decay[..., None, None] + st, state

    init = jnp.zeros((b, G, J, P, N), jnp.float32)
    _, prev = lax.scan(step, init, (jnp.moveaxis(chunk_decay, 1, 0), jnp.moveaxis(chunk_states, 1, 0)))
    prev = jnp.moveaxis(prev, 0, 1)
    y_off = jnp.einsum("bclgn,bcgjpn,bcgjl->bclgjp", cm, prev, jnp.exp(cs))
    y = y_diag + y_off + xs * d_skip.astype(jnp.float32).reshape(G, J, 1)
    y = y.reshape(b, s, SSD_INNER) * jax.nn.silu(z.astype(jnp.float32))
    return rms_norm(y, norm_w)


def setup_inputs(seed: int = 0) -> dict:
    key = jax.random.key(seed)
    ks = jax.random.split(key, 16)
    f32 = jnp.float32
    x = jax.random.normal(ks[0], (BATCH, SEQ, D_MODEL), f32)
    positions = (jnp.arange(SEQ, dtype=jnp.int32)[None, :]
                 + jax.random.randint(ks[1], (BATCH, 1), 0, 4096, dtype=jnp.int32))
    norm_w = 1.0 + 0.02 * jax.random.normal(ks[2], (DEPTH, D_MODEL), f32)
    w_in = jax.random.normal(ks[3], (DEPTH, D_MODEL, D_IN_PROJ), f32) * D_MODEL ** -0.5
    conv_w = jax.random.normal(ks[4], (DEPTH, SSD_CONV, SSD_CONV_CH), f32) * SSD_CONV ** -0.5
    conv_b = 0.02 * jax.random.normal(ks[5], (DEPTH, SSD_CONV_CH), f32)
    dt0 = jnp.exp(jax.random.uniform(ks[6], (DEPTH, SSD_HEADS), f32, math.log(1e-3), math.log(1e-1)))
    dt_bias = dt0 + jnp.log(-jnp.expm1(-dt0))
    a_log = jnp.log(jax.random.uniform(ks[7], (DEPTH, SSD_HEADS), f32, 1.0, 16.0))
    d_skip = 1.0 + 0.02 * jax.random.normal(ks[8], (DEPTH, SSD_HEADS), f32)
    ssd_norm_w = 1.0 + 0.02 * jax.random.normal(ks[9], (DEPTH, SSD_INNER), f32)
    w_attn_br = jax.random.normal(ks[10], (DEPTH, ATTN_OUT_WIDTH, D_MODEL), f32) * ATTN_OUT_WIDTH ** -0.5
    w_ssd_br = jax.random.normal(ks[11], (DEPTH, SSD_INNER, D_MODEL), f32) * SSD_INNER ** -0.5
    gate_b = 0.02 * jax.random.normal(ks[12], (DEPTH, N_BRANCHES, D_MODEL), f32)
    w_out = jax.random.normal(ks[13], (DEPTH, D_MODEL, D_MODEL), f32) * D_MODEL ** -0.5
    final_norm_w = 1.0 + 0.02 * jax.random.normal(ks[14], (D_MODEL,), f32)
    return {"x": x, "positions": positions, "norm_w": norm_w, "w_in": w_in, "conv_w": conv_w,
            "conv_b": conv_b, "dt_bias": dt_bias, "a_log": a_log, "d_skip": d_skip,
            "ssd_norm_w": ssd_norm_w, "w_attn_br": w_attn_br, "w_ssd_br": w_ssd_br,
            "gate_b": gate_b, "w_out": w_out, "final_norm_w": final_norm_w}


def reference(x, positions, norm_w, w_in, conv_w, conv_b, dt_bias, a_log, d_skip, ssd_norm_w,
              w_attn_br, w_ssd_br, gate_b, w_out, final_norm_w):
    b, s, _ = x.shape
    split_idx = [int(v) for v in np.cumsum(SPLIT_SIZES)[:-1]]
    h = x
    for layer in range(DEPTH):
        xn = rms_norm(h, norm_w[layer])
        proj = xn @ w_in[layer]
        q, k, v, z_attn, z_ssd, xbc, dt_raw, gate_logits = jnp.split(proj, split_idx, axis=-1)
        grp = (b, s, N_ATTN_GROUPS, ATTN_HEADS_PER_GROUP, ATTN_HEAD_DIM)
        q = partial_rope(q.reshape(grp), positions)
        k = partial_rope(k.reshape(grp), positions)
        v = v.reshape(grp)
        attn = dilated_mixture_attention(q, k, v)
        y_attn = (attn * jax.nn.silu(z_attn.astype(jnp.float32))).astype(x.dtype) @ w_attn_br[layer]
        ssd = ssd_mixer(xbc, z_ssd, dt_raw, conv_w[layer], conv_b[layer], dt_bias[layer],
                        a_log[layer], d_skip[layer], ssd_norm_w[layer])
        y_ssd = ssd.astype(x.dtype) @ w_ssd_br[layer]
        gates = jax.nn.sigmoid(gate_logits.reshape(b, s, N_BRANCHES, D_MODEL) + gate_b[layer])
        merged = gates[:, :, 0] * y_attn + gates[:, :, 1] * y_ssd
        h = h + merged @ w_out[layer]
    return rms_norm(h, final_norm_w)
```

```python
import math
import numpy as np
import concourse.bass as bass
import concourse.mybir as mybir
from concourse.bass_utils import run_bass_kernel_spmd

F32 = mybir.dt.float32
BF16 = mybir.dt.bfloat16
I32 = mybir.dt.int32
AF = mybir.ActivationFunctionType
ALU = mybir.AluOpType

D = 2048
T = 2048
NB = T // 128
NO = 1024
KC = D // 128
EPS = 1e-5
DIL = (1, 4, 16)
QOFF, KOFF, VOFF, ZAOFF, ZSOFF, XBCOFF, DTOFF, GOFF = 0, 3072, 6144, 9216, 10240, 14336, 20480, 20544
NEG = -30000.0
TWO_PI = 2.0 * math.pi


class _Stop(Exception):
    pass


class Sched:
    ENG = ("pe", "act", "dve", "pool", "sp")

    def __init__(self, nc):
        self.nc = nc
        self.eng = {"pe": nc.tensor, "act": nc.scalar, "dve": nc.vector, "pool": nc.gpsimd, "sp": nc.sync}
        self.sems = {}
        self.cnt = {}
        for e in self.ENG:
            self.sems["E" + e] = nc.alloc_semaphore("prog_" + e)
            self.cnt["E" + e] = 0
        self.seen = {e: {} for e in self.ENG}
        self.lastw = {}
        self.readers = {}

    def dsem(self, name):
        if name not in self.sems:
            self.sems[name] = self.nc.alloc_semaphore(name)
            self.cnt[name] = 0
        return name

    def _wait(self, e, deps):
        best = {}
        for (s, v) in deps:
            if s == "Epe" and e == "pe":
                continue
            if v > best.get(s, 0):
                best[s] = v
        for s, v in best.items():
            if self.seen[e].get(s, 0) >= v:
                continue
            assert v <= self.cnt[s], (e, s, v, self.cnt[s])
            self.eng[e].wait_ge(self.sems[s], v)
            self.seen[e][s] = v

    def _deps(self, reads, writes):
        deps = []
        for k in reads:
            if k in self.lastw:
                deps.append(self.lastw[k])
        for k in writes:
            if k in self.lastw:
                deps.append(self.lastw[k])
            deps.extend(self.readers.get(k, ()))
        return deps

    def _record(self, d, reads, writes):
        for k in reads:
            self.readers.setdefault(k, []).append(d)
        for k in writes:
            self.lastw[k] = d
            self.readers[k] = []

    def op(self, e, fn, reads=(), writes=(), signal=True):
        pr = tuple(k for k in reads if k.startswith("ps"))
        if pr:
            reads = tuple(k for k in reads if not k.startswith("ps"))
            writes = tuple(writes) + pr
        self._wait(e, self._deps(reads, writes))
        ins = fn()
        s = "E" + e
        if signal:
            self.cnt[s] += 1
            ins.then_inc(self.sems[s], 1)
            d = (s, self.cnt[s])
        else:
            d = (s, self.cnt[s] + 1)
        self._record(d, reads, writes)
        return ins

    def dma(self, q, out, in_, reads=(), writes=(), sem="dgen"):
        self._wait(q, self._deps(reads, writes))
        s = self.dsem(sem)
        ins = self.eng[q].dma_start(out=out, in_=in_)
        self.cnt[s] += 16
        ins.then_inc(self.sems[s], 16)
        self._record((s, self.cnt[s]), reads, writes)
        return ins

    def barrier(self):
        for e in self.ENG:
            for s, v in self.cnt.items():
                if v > 0 and s != "E" + e and self.seen[e].get(s, 0) < v:
                    self.eng[e].wait_ge(self.sems[s], v)
                    self.seen[e][s] = v
        self.lastw = {}
        self.readers = {}

    def finish(self):
        for s, v in self.cnt.items():
            if not s.startswith("E") and v > 0:
                self.eng["sp"].wait_ge(self.sems[s], v)


def build(dbg=None, stop=None, wplan=None):
    nc = bass.Bass("TRN2", target_bir_lowering=False)
    S = Sched(nc)
    V, A, P, PE = nc.vector, nc.scalar, nc.gpsimd, nc.tensor

    def din(name, shape, dt=F32):
        return nc.dram_tensor(name, list(shape), dt, kind="ExternalInput").ap()

    x = din("x", [T, D])
    pos_d = din("pos", [32, T], I32)
    w_in = din("w_in", [D, 24640])
    w_ab = din("w_ab", [1024, D])
    w_sb = din("w_sb", [4096, D])
    w_o = din("w_o", [D, D])
    cst_d = din("cst", [128, 1280])
    fnw_d = din("fnw", [128, D])
    out = nc.dram_tensor("out", [NO, D], F32, kind="ExternalOutput").ap()
    ATd = nc.dram_tensor("ATd", [1024, NO], BF16, kind="Internal").ap()
    YGT = nc.dram_tensor("YGT", [4096, NO], BF16, kind="Internal").ap()
    Gd = nc.dram_tensor("Gd", [4096, NO], BF16, kind="Internal").ap()

    def sb(name, shape, dt):
        return nc.alloc_sbuf_tensor(name, list(shape), dt)

    xnT = sb("xnT", [128, KC, T], BF16)
    WST = [sb("wst%d" % i, [128, 8, 256], F32) for i in range(2)]
    WBF = [sb("wbf%d" % i, [128, 16, 256], BF16) for i in range(2)]
    cst = sb("cst_s", [128, 1280], F32)
    cbf = sb("cbf", [128, 1024], BF16)
    stat = sb("stat", [128, 256], F32)
    ones_b = sb("ones_b", [128, 128], BF16)
    ones_f = sb("ones_f", [128, 128], F32)
    ARENA_B = 105504
    arena = sb("arena", [128, ARENA_B // 2], BF16)
    ps = nc.alloc_psum_tensor("ps", [128, 4096], F32)

    C_ID, C_U, C_SL, C_R32, C_MASK = 0, 128, 256, 384, 416
    C_NWT, C_INVF, C_GB, C_CW, C_CB = 672, 688, 689, 721, 913
    C_NS = 961
    C_KBA, C_KBH, C_FLAG = 1216, 1217, 1218
    C_DTB, C_ALOG, C_DSK = 1024, 1088, 1152
    ident_f = cst[:, C_ID:C_ID + 128]
    U_f = cst[:, C_U:C_U + 128]
    SL_f = cst[:, C_SL:C_SL + 128]
    ident = cbf[:, C_ID:C_ID + 128]
    U_b = cbf[:, C_U:C_U + 128]
    SL_b = cbf[:, C_SL:C_SL + 128]
    R32 = cbf[0:32, C_R32:C_R32 + 32]
    I32m = cbf[0:32, C_ID:C_ID + 32]

    def bank(i, n=1):
        return ps[:, i * 512:(i + n) * 512]

    def bkeys(i, n=1):
        return tuple("ps%d" % j for j in range(i, i + n))

    def carve(off, shape, dt):
        n = 1
        for s_ in shape[1:]:
            n *= s_
        assert shape[0] == 128
        if dt == F32:
            ap = arena[:, off // 2: off // 2 + 2 * n].bitcast(F32)
            nb = 4 * n
        elif dt == I32:
            ap = arena[:, off // 2: off // 2 + 2 * n].bitcast(I32)
            nb = 4 * n
        else:
            ap = arena[:, off // 2: off // 2 + n]
            nb = 2 * n
        assert off + nb <= ARENA_B, (off, nb)
        if len(shape) == 3:
            ap = ap.rearrange("p (a b) -> p a b", a=shape[1])
        elif len(shape) == 4:
            ap = ap.rearrange("p (a b c) -> p a b c", a=shape[1], b=shape[2])
        return ap

    S.dma("sp", cst[:, :], cst_d[:, :], writes=("cst",), sem="dconst")
    S.op("dve", lambda: V.tensor_copy(out=cbf[:, 0:672], in_=cst[:, 0:672]), reads=("cst",), writes=("cbf",))
    S.op("dve", lambda: V.memset(ones_b[:, :], 1.0), writes=("ones_b",))
    S.op("dve", lambda: V.memset(ones_f[:, :], 1.0), writes=("ones_f",))
    CK = ("cst", "cbf", "ones_b", "ones_f")

    wctr = [0, 0]
    wsrc = {"w_in": w_in, "w_ab": w_ab, "w_sb": w_sb, "w_o": w_o}
    wrec = []
    wstate = {"i": 0, "ready": None}

    def _issue_w(spec, slot):
        (sname, r0, nkc, segs, scale_col) = spec
        src = wsrc[sname]
        wb, kb = WBF[slot], "wbf%d" % slot
        for h0 in range(0, nkc, 8):
            n = min(8, nkc - h0)
            st = wctr[0] % 2
            wctr[0] += 1
            ks = "wst%d" % st
            c = 0
            for (c0, ncols) in segs:
                S.dma("sp", WST[st][:, 0:n, c:c + ncols],
                      src[(r0 + h0) * 128:(r0 + h0 + n) * 128, c0:c0 + ncols].rearrange("(k p) c -> p k c", p=128),
                      writes=(ks,), sem="dw%d" % st)
                c += ncols
            if scale_col is None:
                S.op("dve", lambda: V.tensor_copy(out=wb[:, h0:h0 + n, 0:c], in_=WST[st][:, 0:n, 0:c]), reads=(ks,), writes=(kb,))
            else:
                sc = cst[:, scale_col + r0 + h0: scale_col + r0 + h0 + n]
                S.op("dve", lambda: V.tensor_tensor(out=wb[:, h0:h0 + n, 0:c], in0=WST[st][:, 0:n, 0:c],
                                                    in1=sc.unsqueeze(2).to_broadcast([128, n, c]), op=ALU.mult),
                     reads=(ks, "cst"), writes=(kb,))
        return wb, kb

    def load_w(sname, r0, nkc, segs, scale_col=None):
        spec = (sname, r0, nkc, tuple(segs), scale_col)
        i = wstate["i"]
        wstate["i"] += 1
        if wplan is None:
            wrec.append(spec)
            return _issue_w(spec, i % 2)
        assert wplan[i] == spec, (i, wplan[i], spec)
        cur = wstate["ready"] if wstate["ready"] is not None else _issue_w(spec, i % 2)
        wstate["ready"] = _issue_w(wplan[i + 1], (i + 1) % 2) if i + 1 < len(wplan) else None
        return cur

    pfctr = [0]
    pendB = []

    def flushB(keep=0):
        while len(pendB) > keep:
            pendB.pop(0)()

    def proj_fm(wb, kb, ct, nkc_list, rhs_fn, rkey, tiles=(0, 1), b0=None, first=True, last=True):
        if b0 is None:
            slot = pfctr[0] % 2
            pfctr[0] += 1
            b0 = slot * 2
        tot = sum(p[2] for p in nkc_list)
        i = 0
        for (wb_, kb_, nkc, koff) in nkc_list:
            for kc in range(nkc):
                for tt in tiles:
                    S.op("pe", lambda: PE.matmul(out=bank(b0 + tt), lhsT=wb_[:, kc, ct * 128:(ct + 1) * 128],
                                                 rhs=rhs_fn(koff + kc, tt), start=(first and i == 0), stop=(last and i == tot - 1)),
                         reads=(kb_, rkey), writes=bkeys(b0 + tt), signal=(i == tot - 1))
                i += 1
        return bank(b0, 2), bkeys(b0 + tiles[0], len(tiles))

    Ctab = carve(45056, [128, T], F32)
    Stab = carve(53248, [128, T], F32)
    tmpf = carve(67584, [128, T], F32)
    kscr = carve(24576, [128, T], F32)
    posi = tmpf.bitcast(I32)
    S.dma("sp", posi[0:32, :], pos_d[:, :], writes=("tmpf",), sem="dpos")
    ang = Ctab
    S.op("dve", lambda: V.tensor_copy(out=ang[0:32, :], in_=posi[0:32, :]), reads=("tmpf",), writes=("Ctab",))
    S.op("dve", lambda: V.tensor_scalar(out=ang[0:32, :], in0=ang[0:32, :], scalar1=cst[0:32, C_INVF:C_INVF + 1], scalar2=1.0 / TWO_PI,
                                        op0=ALU.mult, op1=ALU.mult), reads=("Ctab", "cst"), writes=("Ctab",))

    def frac_sin(dst, dkey, shift):
        raise NotImplementedError

    u_ap = ang[0:32, :]
    tf = tmpf[0:32, :]
    ti = tmpf.bitcast(I32)[0:32, :]
    st_ap = Stab[0:32, :]
    for which in ("sin", "cos"):
        shift = 0.0 if which == "sin" else 0.25
        dst = st_ap if which == "sin" else u_ap
        dkey = "Stab" if which == "sin" else "Ctab"
        S.op("dve", lambda: V.tensor_scalar(out=dst, in0=u_ap, scalar1=shift, scalar2=None, op0=ALU.add), reads=("Ctab",), writes=(dkey,))
        S.op("dve", lambda: V.tensor_copy(out=ti, in_=dst), reads=(dkey,), writes=("tmpf",))
        qf = qcs.bitcast(F32) if False else None
        kf = kscr[0:32, :]
        S.op("dve", lambda: V.tensor_copy(out=kf, in_=ti), reads=("tmpf",), writes=("Kp",))
        S.op("dve", lambda: V.tensor_tensor(out=dst, in0=dst, in1=kf, op=ALU.subtract), reads=(dkey, "Kp"), writes=(dkey,))
        S.op("dve", lambda: V.tensor_scalar(out=kf, in0=dst, scalar1=0.0, scalar2=None, op0=ALU.is_lt), reads=(dkey,), writes=("Kp",))
        S.op("dve", lambda: V.tensor_tensor(out=dst, in0=dst, in1=kf, op=ALU.add), reads=(dkey, "Kp"), writes=(dkey,))
        S.op("dve", lambda: V.tensor_scalar(out=kf, in0=dst, scalar1=1.0, scalar2=None, op0=ALU.is_ge), reads=(dkey,), writes=("Kp",))
        S.op("dve", lambda: V.tensor_tensor(out=dst, in0=dst, in1=kf, op=ALU.subtract), reads=(dkey, "Kp"), writes=(dkey,))
        S.op("act", lambda: A.activation(out=dst, in_=dst, func=AF.Sin, scale=-6.283184, bias=3.141592), reads=(dkey,), writes=(dkey,))


    xin_s = [carve(i * 8192, [128, D], F32) for i in range(3)]
    xsb_s = [carve(75776 + i * 4096, [128, D], BF16) for i in range(2)]
    nwT = cst[:, C_NWT:C_NWT + KC]
    for tb in range(NB):
        sl = tb % 2
        s3 = tb % 3
        xin, kx, xsb, kxs = xin_s[s3], "xin%d" % s3, xsb_s[sl], "xsb%d" % sl
        S.dma("sp", xin, x[tb * 128:(tb + 1) * 128, :], writes=(kx,), sem="dx%d" % s3)
        ssc = stat[:, tb:tb + 1]
        S.op("act", lambda: A.activation(out=xsb, in_=xin, func=AF.Square, accum_out=ssc), reads=(kx,), writes=(kxs, "ss%d" % tb))
        rs = stat[:, 16 + tb:17 + tb]
        S.op("dve", lambda: V.tensor_scalar(out=rs, in0=ssc, scalar1=1.0 / D, scalar2=EPS, op0=ALU.mult, op1=ALU.add),
             reads=("ss%d" % tb,), writes=("rs%d" % tb,))
        S.op("act", lambda: A.activation(out=rs, in_=rs, func=AF.Sqrt), reads=("rs%d" % tb,), writes=("rs%d" % tb,))
        S.op("dve", lambda: V.reciprocal(out=rs, in_=rs), reads=("rs%d" % tb,), writes=("rs%d" % tb,))
        S.op("dve", lambda: V.tensor_scalar(out=xsb, in0=xin, scalar1=rs, scalar2=None, op0=ALU.mult),
             reads=(kx, "rs%d" % tb), writes=(kxs,))
        pb = 4 + 2 * sl
        psb = bank(pb, 2).bitcast(BF16)
        for j in range(KC):
            S.op("pe", lambda: PE.transpose(out=psb[:, j * 128:(j + 1) * 128], in_=xsb[:, j * 128:(j + 1) * 128], identity=ident),
                 reads=(kxs, "cbf"), writes=bkeys(pb, 2), signal=(j == KC - 1))
        S.op("dve", lambda: V.tensor_tensor(out=xnT[:, :, tb * 128:(tb + 1) * 128], in0=psb.rearrange("p (k t) -> p k t", k=KC),
                                            in1=nwT.unsqueeze(2).to_broadcast([128, KC, 128]), op=ALU.mult),
             reads=bkeys(pb, 2) + ("cst",), writes=("xnT",))

    def xn_rhs(kc, tt, t0):
        return xnT[:, kc, t0 + tt * 512: t0 + (tt + 1) * 512]

    dbg_done = [False]

    def dbg_dump(name, ap2d, keys, dt):
        if dbg == name and not dbg_done[0]:
            dbg_done[0] = True
            shp = list(ap2d.shape)
            o = nc.dram_tensor("dbg", shp, dt, kind="ExternalOutput").ap()
            S.dma("sp", o, ap2d, reads=keys, sem="ddbg")
            raise _Stop()

    def phases():

        S.barrier()
        accden = carve(0, [128, 2, 2, NO], F32)
        kfull = carve(0, [128, 4096], F32)
        Qp = carve(16384, [128, 2, T], BF16)
        Kp = carve(24576, [128, 2, T], BF16)
        Vt = carve(32768, [128, 16, 256], BF16)
        zs_s = [carve(40960, [128, 2, NO], BF16), carve(75776, [128, 2, NO], BF16)]
        qcs = carve(61440, [128, 2, 1024], BF16)
        PT = [carve(65536 + i * 1024, [128, 512], BF16) for i in range(2)]

        masks = cbf[:, C_MASK:C_MASK + 256]
        SCALE = 128.0 ** -0.5
        sctr = [0]

        kb_all = cst[:, C_KBA:C_KBA + 1]
        kb_half = cst[:, C_KBH:C_KBH + 1]

        def qk_proj(dst, dk, wb, kb, ct, d, t_lo, t_hi):
            th = t_lo // 1024
            lo, hi = t_lo - th * 1024, t_hi - th * 1024
            tiles = tuple(range(lo // 512, hi // 512))
            n = hi - lo
            pf, pk = proj_fm(wb, kb, ct, [(wb, kb, 16, 0)], lambda kc, tt: xn_rhs(kc, tt, th * 1024), "xnT", tiles=tiles)
            flushB()
            pfs = pf[:, lo:hi]
            m0, m1 = t_lo // d, t_hi // d
            dview = dst[:, ct, :].rearrange("p (r m) -> p r m", r=d)
            S.op("act", lambda: A.activation(out=dview[:, :, m0:m1], in_=pfs.rearrange("p (m r) -> p r m", r=d), func=AF.Copy), reads=pk, writes=(dk,))
            S.op("dve", lambda: V.tensor_tensor(out=qcs[0:32, 0, 0:n], in0=pfs[0:32, :], in1=Ctab[0:32, t_lo:t_hi], op=ALU.mult), reads=pk + ("Ctab",), writes=("qc",))
            S.op("dve", lambda: V.tensor_tensor(out=qcs[0:32, 1, 0:n], in0=pfs[0:32, :], in1=Stab[0:32, t_lo:t_hi], op=ALU.mult), reads=pk + ("Stab",), writes=("qs",))
            nt = len(tiles)

            def stageB():
                for j in range(nt):
                    S.op("pe", lambda: PE.matmul(out=bank(4 + j)[0:32, :], lhsT=I32m, rhs=qcs[0:32, 0, j * 512:(j + 1) * 512], start=True, stop=False),
                         reads=("qc", "cbf"), writes=bkeys(4 + j), signal=False)
                    S.op("pe", lambda: PE.matmul(out=bank(4 + j)[0:32, :], lhsT=R32, rhs=qcs[0:32, 1, j * 512:(j + 1) * 512], start=False, stop=True),
                         reads=("qs", "cbf"), writes=bkeys(4 + j), signal=True)
                S.op("act", lambda: A.activation(out=dview[0:32, :, m0:m1], in_=bank(4, nt)[0:32, :].rearrange("p (m r) -> p r m", r=d), func=AF.Copy),
                     reads=bkeys(4, nt), writes=(dk,))
            pendB.append(stageB)

        def attn_block(g, d, blk, r, n, q0, npc, prev_zero, half_zero):
            nq = 128 - q0
            si = sctr[0] % 2
            sctr[0] += 1
            sbk, sk = bank(4 + si), bkeys(4 + si)
            obk, ok = bank(6 + si), bkeys(6 + si)
            for pc in range(npc):
                for hh in range(2):
                    col = (pc * 2 + hh) * nq
                    S.op("pe", lambda: PE.matmul(out=sbk[:, col:col + nq], lhsT=ident, rhs=masks[:, pc * 128 + q0:(pc + 1) * 128], start=(pc == 0 and hh == 0), stop=False,
                                                 skip_group_check=True), reads=("cbf",), writes=sk, signal=False)
            for pc in range(npc):
                kb_ = blk - pc
                for hh in range(2):
                    col = (pc * 2 + hh) * nq
                    S.op("pe", lambda: PE.matmul(out=sbk[:, col:col + nq], lhsT=Kp[:, hh, kb_ * 128:(kb_ + 1) * 128], rhs=Qp[:, hh, blk * 128 + q0:(blk + 1) * 128],
                                                 start=False, stop=True, skip_group_check=True),
                         reads=("Kp", "Qp"), writes=sk, signal=(pc == npc - 1 and hh == 1))
            pt, pk_ = PT[si], "PT%d" % si
            w2 = 2 * nq
            if half_zero:
                S.op("act", lambda: A.activation(out=pt[:, 0:npc * w2], in_=sbk[:, 0:npc * w2], func=AF.Exp, scale=SCALE, bias=kb_half), reads=sk + ("cst",), writes=(pk_,))
            elif prev_zero:
                S.op("act", lambda: A.activation(out=pt[:, 0:w2], in_=sbk[:, 0:w2], func=AF.Exp, scale=SCALE), reads=sk, writes=(pk_,))
                S.op("act", lambda: A.activation(out=pt[:, w2:2 * w2], in_=sbk[:, w2:2 * w2], func=AF.Exp, scale=SCALE, bias=kb_all), reads=sk + ("cst",), writes=(pk_,))
            else:
                S.op("act", lambda: A.activation(out=pt[:, 0:npc * w2], in_=sbk[:, 0:npc * w2], func=AF.Exp, scale=SCALE), reads=sk, writes=(pk_,))
            idx = 0
            for hh in range(2):
                for pc in range(npc):
                    kb_ = blk - pc
                    col = (pc * 2 + hh) * nq
                    S.op("pe", lambda: PE.matmul(out=obk[:, hh * nq:(hh + 1) * nq], lhsT=Vt[:, kb_, hh * 128:(hh + 1) * 128], rhs=pt[:, col:col + nq],
                                                 start=(idx == 0), stop=(pc == npc - 1), skip_group_check=True), reads=("Vt", pk_), writes=ok, signal=False)
                    idx += 1
            for pc in range(npc):
                S.op("pe", lambda: PE.matmul(out=obk[:, 256:256 + w2], lhsT=ones_b[:, :], rhs=pt[:, pc * w2:(pc + 1) * w2], start=False, stop=(pc == npc - 1),
                                             skip_group_check=True), reads=("ones_b", pk_), writes=ok, signal=(pc == npc - 1))
            i0 = r + d * (128 * n + q0) - (T - NO)
            dst = accden[:, :, :, i0:i0 + d * (nq - 1) + 1:d]
            src = obk.rearrange("p (k x) -> p k x", k=2)[:, :, 0:w2].rearrange("p k (h q) -> p k h q", h=2)
            if g == 2:
                S.op("dve", lambda: V.tensor_copy(out=dst, in_=src), reads=ok, writes=("accden",))
            else:
                S.op("dve", lambda: V.tensor_tensor(out=dst, in0=dst, in1=src, op=ALU.add), reads=ok + ("accden",), writes=("accden",))

        pend_fin = []

        def attn_headbatch(hb):
            h0 = 2 * hb
            zs, zk = zs_s[hb % 2], "zs%d" % (hb % 2)
            wb, kb = load_w("w_in", 0, 16, [(ZAOFF + h0 * 128, 256)])
            for ct in range(2):
                pf, pk = proj_fm(wb, kb, ct, [(wb, kb, 16, 0)], lambda kc, tt: xn_rhs(kc, tt, 1024), "xnT")
                S.op("act", lambda: A.activation(out=zs[:, ct, :], in_=pf, func=AF.Silu), reads=pk, writes=(zk,))
            while pend_fin:
                pend_fin.pop(0)()
            for g in (2, 1, 0):
                d = DIL[g]
                nbr = 16 // d
                wb, kb = load_w("w_in", 0, 16, [(QOFF + g * 1024 + h0 * 128, 256)])
                for ct in range(2):
                    qk_proj(Qp, "Qp", wb, kb, ct, d, 1024, 2048)
                wb, kb = load_w("w_in", 0, 16, [(KOFF + g * 1024 + h0 * 128, 256)])
                for ct in range(2):
                    qk_proj(Kp, "Kp", wb, kb, ct, d, 0 if g == 2 else 512, 1024)
                    qk_proj(Kp, "Kp", wb, kb, ct, d, 1024, 2048)
                if g == 2:
                    vblks = list(range(16))
                elif g == 1:
                    vblks = [r * 4 + n for r in range(4) for n in (1, 2, 3)]
                else:
                    vblks = list(range(7, 16))
                wb, kb = load_w("w_in", 0, 16, [(VOFF + g * 1024 + h0 * 128, 256)])
                for b4 in range(0, len(vblks), 4):
                    sub = vblks[b4:b4 + 4]
                    slot = pfctr[0] % 2
                    pfctr[0] += 1
                    pb = slot * 2
                    for bi, blk in enumerate(sub):
                        r, n = blk // nbr, blk % nbr
                        t0 = r + d * 128 * n
                        oap = bank(pb + bi // 2)[:, (bi % 2) * 256:(bi % 2) * 256 + 256]
                        for kc in range(KC):
                            S.op("pe", lambda: PE.matmul(out=oap, lhsT=xnT[:, kc, t0:t0 + d * 127 + 1:d], rhs=wb[:, kc, 0:256], start=(kc == 0), stop=(kc == KC - 1)),
                                 reads=("xnT", kb), writes=bkeys(pb + bi // 2), signal=(kc == KC - 1))
                    flushB()
                    runs = []
                    for bi, blk in enumerate(sub):
                        if runs and runs[-1][1] + runs[-1][2] == blk and runs[-1][0] + runs[-1][2] == bi:
                            runs[-1][2] += 1
                        else:
                            runs.append([bi, blk, 1])
                    for (bi, blk, cnt) in runs:
                        S.op("act", lambda: A.activation(out=Vt[:, blk:blk + cnt, :], in_=bank(pb, 2)[:, bi * 256:(bi + cnt) * 256].rearrange("p (b c) -> p b c", b=cnt),
                                                         func=AF.Copy), reads=bkeys(pb, 2), writes=("Vt",))
                if g == 2:
                    for r in range(16):
                        attn_block(g, d, r, r, 0, 64, 1, False, True)
                elif g == 1:
                    for r in range(4):
                        for n in (2, 3):
                            attn_block(g, d, r * 4 + n, r, n, 0, 2, n == 2, False)
                else:
                    for n in range(8, 16):
                        attn_block(g, d, n, 0, n, 0, 2, n == 8, False)
            def fin():
                S.op("dve", lambda: V.reciprocal(out=accden[:, 1, :, :], in_=accden[:, 1, :, :]), reads=("accden",), writes=("accden",))
                S.op("dve", lambda: V.tensor_tensor(out=accden[:, 0, :, :], in0=accden[:, 0, :, :], in1=accden[:, 1, :, :], op=ALU.mult), reads=("accden",), writes=("accden",))
                S.op("dve", lambda: V.tensor_tensor(out=zs[:, :, :], in0=accden[:, 0, :, :], in1=zs[:, :, :], op=ALU.mult), reads=("accden", zk), writes=(zk,))
                S.dma("sp", ATd[h0 * 128:(h0 + 2) * 128, :].rearrange("(h p) t -> p h t", p=128), zs[:, :, :], reads=(zk,), writes=("ATd",), sem="dsp_a%d" % (hb % 2))
            pend_fin.append(fin)

        skip_attn = (stop == "noattn")
        if not skip_attn:
            for hb in range(4):
                attn_headbatch(hb)
            while pend_fin:
                pend_fin.pop(0)()

        S.barrier()
        xraw = carve(0, [128, 2056], F32)
        cacc = carve(8224, [128, T], F32)
        xact2 = [carve(16416, [128, T], BF16), carve(100384, [128, T], BF16)]
        BT = carve(20512, [128, T], BF16)
        CT = carve(24608, [128, T], BF16)
        xs_tm = carve(28704, [128, 16, 512], BF16)
        B_tm = carve(45088, [128, 16, 128], BF16)
        z_tm = carve(49184, [128, 8, 512], BF16)
        dtr = carve(65568, [128, 16, 64], F32)
        dtv = carve(69664, [128, 5, 128], F32)
        dahl = carve(72224, [128, 2, 128], BF16)
        tmp128 = carve(72736, [128, 128], F32)
        st = carve(73248, [128, 512], F32)
        st_bf = carve(75296, [128, 512], BF16)
        daU = carve(76320, [128, 2, 8, 128], BF16)
        Dexp = carve(80416, [128, 8, 128], F32)
        MT = carve(84512, [128, 8, 128], BF16)
        CBm = carve(86560, [128, 128], F32)
        xdt = carve(87072, [128, 512], BF16)
        xdtd2 = [carve(88096, [128, 512], BF16), carve(98336, [128, 512], BF16), carve(99360, [128, 512], BF16)]
        t1 = carve(89120, [128, 512], F32)
        t2 = carve(91168, [128, 512], BF16)
        xsd = carve(92192, [128, 512], BF16)
        ygb = carve(93216, [128, 512], BF16)
        ygT_st = carve(94240, [128, 4, 512], BF16)
        Aneg = stat[:, 64:128]
        sq = stat[:, 128:192]

        S.op("dve", lambda: V.memset(xraw[:, 0:3], 0.0), writes=("xraw",))
        S.op("dve", lambda: V.memset(sq, 0.0), writes=("sq",))
        S.op("act", lambda: A.activation(out=Aneg, in_=cst[:, C_ALOG:C_ALOG + 64], func=AF.Exp), reads=("cst",), writes=("Aneg",))
        S.op("dve", lambda: V.tensor_scalar(out=Aneg, in0=Aneg, scalar1=-1.0, scalar2=None, op0=ALU.mult), reads=("Aneg",), writes=("Aneg",))
        wb, kb = load_w("w_in", 0, 16, [(DTOFF, 64)])
        for c8 in range(2):
            slot = pfctr[0] % 2
            pfctr[0] += 1
            pb = slot * 2
            for ci in range(8):
                c = c8 * 8 + ci
                for kc in range(KC):
                    S.op("pe", lambda: PE.matmul(out=bank(pb)[:, ci * 64:(ci + 1) * 64], lhsT=xnT[:, kc, c * 128:(c + 1) * 128], rhs=wb[:, kc, 0:64],
                                                 start=(kc == 0), stop=(kc == KC - 1)),
                         reads=("xnT", kb), writes=bkeys(pb), signal=(kc == KC - 1))
            S.op("act", lambda: A.activation(out=dtr[:, c8 * 8:(c8 + 1) * 8, :], in_=bank(pb).rearrange("p (c j) -> p c j", c=8), func=AF.Copy),
                 reads=bkeys(pb), writes=("dtr",))

        def conv_silu(cti, dst_ap, dkey, t_lo=0):
            n = T - t_lo
            S.op("act", lambda: A.activation(out=cacc[:, t_lo:T], in_=xraw[:, 3 + t_lo:3 + T], func=AF.Identity, scale=cst[:, C_CW + cti * 4 + 3:C_CW + cti * 4 + 4],
                                             bias=cst[:, C_CB + cti:C_CB + cti + 1]), reads=("xraw", "cst"), writes=("cacc",))
            for k in range(3):
                S.op("dve", lambda: V.scalar_tensor_tensor(out=cacc[:, t_lo:T], in0=xraw[:, k + t_lo:k + T], scalar=cst[:, C_CW + cti * 4 + k:C_CW + cti * 4 + k + 1],
                                                           in1=cacc[:, t_lo:T], op0=ALU.mult, op1=ALU.add), reads=("xraw", "cst", "cacc"), writes=("cacc",))
            S.op("act", lambda: A.activation(out=dst_ap[:, t_lo:T], in_=cacc[:, t_lo:T], func=AF.Silu), reads=("cacc",), writes=(dkey,))

        def proj_to_xraw(wb, kb, ct):
            for th in range(2):
                pf, pk = proj_fm(wb, kb, ct, [(wb, kb, 16, 0)], lambda kc, tt: xn_rhs(kc, tt, th * 1024), "xnT")
                if th == 1:
                    flushB(keep=1)
                S.op("act", lambda: A.activation(out=xraw[:, 3 + th * 1024:3 + (th + 1) * 1024], in_=pf, func=AF.Copy), reads=pk, writes=("xraw",))

        trc = [0]

        def transpose_to_tm(src, skey, dst3, dkey):
            pb = 4 + 2 * (trc[0] % 2)
            trc[0] += 1
            psb = bank(pb, 2).bitcast(BF16)
            for c in range(16):
                S.op("pe", lambda: PE.transpose(out=psb[:, c * 128:(c + 1) * 128], in_=src[:, c * 128:(c + 1) * 128], identity=ident),
                     reads=(skey, "cbf"), writes=bkeys(pb, 2), signal=(c == 15))
            S.op("act", lambda: A.activation(out=dst3, in_=psb.rearrange("p (c f) -> p c f", c=16), func=AF.Copy), reads=bkeys(pb, 2), writes=(dkey,))

        def ssd_group(g):
            for wt in range(2):
                wb, kb = load_w("w_in", 0, 16, [(XBCOFF + g * 512 + wt * 256, 256)])
                for ct in range(2):
                    i = wt * 2 + ct
                    proj_to_xraw(wb, kb, ct)
                    xa, xk = xact2[i % 2], "xact%d" % (i % 2)
                    conv_silu(g * 4 + i, xa, xk)
                    pendB.append(lambda i=i, xa=xa, xk=xk: transpose_to_tm(xa, xk, xs_tm[:, :, i * 128:(i + 1) * 128], "xs_tm"))
            wb, kb = load_w("w_in", 0, 16, [(XBCOFF + 4096 + g * 128, 128), (XBCOFF + 5120 + g * 128, 128)])
            proj_to_xraw(wb, kb, 0)
            conv_silu(32 + g, BT, "BT")
            pendB.append(lambda: transpose_to_tm(BT, "BT", B_tm[:, :, :], "B_tm"))
            hb_ = pfctr[0] % 4
            pfctr[0] += 1
            for kc in range(KC):
                S.op("pe", lambda: PE.matmul(out=bank(hb_)[:, 0:128], lhsT=wb[:, kc, 128:256], rhs=xnT[:, kc, T - NO - 128:T - NO], start=(kc == 0), stop=(kc == KC - 1)),
                     reads=(kb, "xnT"), writes=bkeys(hb_), signal=(kc == KC - 1))
            S.op("act", lambda: A.activation(out=xraw[:, 3 + T - NO - 3:3 + T - NO], in_=bank(hb_)[:, 125:128], func=AF.Copy), reads=bkeys(hb_), writes=("xraw",))
            pf, pk = proj_fm(wb, kb, 1, [(wb, kb, 16, 0)], lambda kc, tt: xn_rhs(kc, tt, T - NO), "xnT")
            flushB(keep=1)
            S.op("act", lambda: A.activation(out=xraw[:, 3 + T - NO:3 + T], in_=pf, func=AF.Copy), reads=pk, writes=("xraw",))
            conv_silu(40 + g, CT, "CT", t_lo=T - NO)
            for wt in range(2):
                wb, kb = load_w("w_in", 0, 16, [(ZSOFF + g * 512 + wt * 256, 256)])
                for c2 in range(4, 8):
                    bz = pfctr[0] % 4
                    pfctr[0] += 1
                    for ci in range(2):
                        c = c2 * 2 + ci
                        for kc in range(KC):
                            S.op("pe", lambda: PE.matmul(out=bank(bz)[:, ci * 256:(ci + 1) * 256], lhsT=xnT[:, kc, c * 128:(c + 1) * 128], rhs=wb[:, kc, 0:256],
                                                         start=(kc == 0), stop=(kc == KC - 1)),
                                 reads=("xnT", kb), writes=bkeys(bz), signal=(kc == KC - 1))
                    S.op("act", lambda: A.activation(out=z_tm[:, c2 * 2 - 8:c2 * 2 - 6, wt * 256:(wt + 1) * 256], in_=bank(bz).rearrange("p (c f) -> p c f", c=2),
                                                     func=AF.Silu), reads=bkeys(bz), writes=("z_tm",))
            flushB()
            g8 = slice(g * 8, (g + 1) * 8)
            dt3 = dtv[:, 0, :].rearrange("p (c j) -> p c j", c=16)
            S.op("dve", lambda: V.tensor_tensor(out=dt3, in0=dtr[:, :, g8], in1=cst[:, C_DTB + g * 8:C_DTB + g * 8 + 8].unsqueeze(1).to_broadcast([128, 16, 8]),
                                                op=ALU.add), reads=("dtr", "cst"), writes=("dtv",))
            S.op("act", lambda: A.activation(out=dtv[:, 0, :], in_=dtv[:, 0, :], func=AF.Exp), reads=("dtv",), writes=("dtv",))
            S.op("act", lambda: A.activation(out=dtv[:, 0, :], in_=dtv[:, 0, :], func=AF.Ln, bias=1.0), reads=("dtv",), writes=("dtv",))
            da3 = dtv[:, 1, :].rearrange("p (c j) -> p c j", c=16)
            S.op("dve", lambda: V.tensor_tensor(out=da3, in0=dt3, in1=Aneg[:, g8].unsqueeze(1).to_broadcast([128, 16, 8]), op=ALU.mult),
                 reads=("dtv", "Aneg"), writes=("dtv",))
            bq = 4
            for (i, lh) in enumerate((U_f, SL_f, ones_f[:, :])):
                S.op("pe", lambda: PE.matmul(out=bank(bq)[:, i * 128:(i + 1) * 128], lhsT=lh, rhs=dtv[:, 1, :], start=True, stop=True),
                     reads=("dtv", "cst", "ones_f"), writes=bkeys(bq), signal=(i == 2))
            S.op("act", lambda: A.activation(out=dtv[:, 2:5, :], in_=bank(bq)[:, 0:384].rearrange("p (a b) -> p a b", a=3), func=AF.Exp),
                 reads=bkeys(bq), writes=("dtv",))
            S.op("dve", lambda: V.tensor_tensor(out=dtv[:, 3, :], in0=dtv[:, 3, :], in1=dtv[:, 0, :], op=ALU.mult), reads=("dtv",), writes=("dtv",))
            S.op("dve", lambda: V.tensor_copy(out=dahl[:, 0, :], in_=dtv[:, 1, :]), reads=("dtv",), writes=("dahl",))
            S.op("dve", lambda: V.tensor_copy(out=tmp128, in_=dahl[:, 0, :]), reads=("dahl",), writes=("tmp128",))
            S.op("dve", lambda: V.tensor_tensor(out=dahl[:, 1, :], in0=dtv[:, 1, :], in1=tmp128, op=ALU.subtract), reads=("dtv", "tmp128"), writes=("dahl",))
            S.op("dve", lambda: V.memset(st, 0.0), writes=("st",))
            S.op("dve", lambda: V.memset(st_bf, 0.0), writes=("st_bf",))
            def xs3_of(c):
                return xs_tm[:, c, :].rearrange("p (j q) -> p j q", j=8)

            def mk_xdtd(c):
                c8 = slice(c * 8, (c + 1) * 8)
                S.op("pool", lambda: P.tensor_tensor(out=xdtd2[c % 3].rearrange("p (j q) -> p j q", j=8), in0=xs3_of(c),
                                                     in1=dtv[:, 3, c8].unsqueeze(2).to_broadcast([128, 8, 64]), op=ALU.mult),
                     reads=("xs_tm", "dtv"), writes=("xdtd%d" % (c % 3),))

            def state_update(c):
                c8 = slice(c * 8, (c + 1) * 8)
                S.op("pe", lambda: PE.matmul(out=bank(2), lhsT=B_tm[:, c, :], rhs=xdtd2[c % 3], start=True, stop=True), reads=("B_tm", "xdtd%d" % (c % 3)), writes=bkeys(2))
                S.op("pool", lambda: P.tensor_tensor(out=st.rearrange("p (j q) -> p j q", j=8), in0=st.rearrange("p (j q) -> p j q", j=8),
                                                     in1=dtv[:, 4, c8].unsqueeze(2).to_broadcast([128, 8, 64]), op=ALU.mult), reads=("st", "dtv"), writes=("st",))
                S.op("dve", lambda: V.tensor_tensor(out=st, in0=st, in1=bank(2), op=ALU.add), reads=bkeys(2) + ("st",), writes=("st",))
                if c == 7:
                    S.op("dve", lambda: V.tensor_scalar(out=st, in0=st, scalar1=cst[:, C_FLAG:C_FLAG + 1], scalar2=None, op0=ALU.mult), reads=("st", "cst"), writes=("st",))
                S.op("act", lambda: A.activation(out=st_bf, in_=st, func=AF.Copy), reads=("st",), writes=("st_bf",))

            def front_a(c):
                cs = slice(c * 128, (c + 1) * 128)
                c8 = slice(c * 8, (c + 1) * 8)
                yb = (7, 3, 0)[c % 3]
                S.op("pe", lambda: PE.matmul(out=bank(4)[:, 0:128], lhsT=BT[:, cs], rhs=CT[:, cs], start=True, stop=True), reads=("BT", "CT"), writes=bkeys(4))
                S.op("dve", lambda: V.tensor_tensor(out=CBm, in0=bank(4)[:, 0:128], in1=U_f, op=ALU.mult), reads=bkeys(4) + ("cst",), writes=("CBm",))
                for hl in range(2):
                    eng, E = ("dve", V) if hl == 0 else ("pool", P)
                    S.op(eng, lambda: E.tensor_tensor(out=daU[:, hl, :, :], in0=U_b.unsqueeze(1).to_broadcast([128, 8, 128]),
                                                      in1=dahl[:, hl, c8].unsqueeze(2).to_broadcast([128, 8, 128]), op=ALU.mult),
                         reads=("cbf", "dahl"), writes=("daU%d" % hl,))
                for half in range(2):
                    for hl in range(2):
                        S.op("pe", lambda: PE.matmul(out=bank(5 + half), lhsT=SL_b, rhs=daU[:, hl, half * 4:(half + 1) * 4, :], start=(hl == 0), stop=(hl == 1)),
                             reads=("cbf", "daU%d" % hl), writes=bkeys(5 + half), signal=(hl == 1))
                S.op("act", lambda: A.activation(out=Dexp, in_=bank(5, 2).rearrange("p (j l) -> p j l", j=8), func=AF.Exp), reads=bkeys(5, 2), writes=("Dexp",))
                S.op("dve", lambda: V.tensor_tensor(out=MT, in0=Dexp, in1=CBm.unsqueeze(1).to_broadcast([128, 8, 128]), op=ALU.mult),
                     reads=("Dexp", "CBm"), writes=("MT",))
                S.op("pool", lambda: P.tensor_tensor(out=xdt.rearrange("p (j q) -> p j q", j=8), in0=xs3_of(c),
                                                     in1=dtv[:, 0, c8].unsqueeze(2).to_broadcast([128, 8, 64]), op=ALU.mult), reads=("xs_tm", "dtv"), writes=("xdt",))
                mk_xdtd(c)
                S.op("pool", lambda: P.tensor_tensor(out=xsd.rearrange("p (j q) -> p j q", j=8), in0=xs3_of(c),
                                                     in1=cst[:, C_DSK + g * 8:C_DSK + g * 8 + 8].unsqueeze(2).to_broadcast([128, 8, 64]), op=ALU.mult),
                     reads=("xs_tm", "cst"), writes=("xsd",))

            def front_b(c):
                yb = (7, 3, 0)[c % 3]
                for j in range(8):
                    S.op("pe", lambda: PE.matmul(out=bank(yb)[:, j * 64:(j + 1) * 64], lhsT=MT[:, j, :], rhs=xdt[:, j * 64:(j + 1) * 64], start=(j == 0), stop=False,
                                                 skip_group_check=True),
                         reads=("MT", "xdt"), writes=bkeys(yb), signal=False)
                S.op("pe", lambda: PE.matmul(out=bank(yb), lhsT=ident, rhs=xsd, start=False, stop=True, skip_group_check=True),
                     reads=("cbf", "xsd"), writes=bkeys(yb), signal=True)

            def back(c):
                cs = slice(c * 128, (c + 1) * 128)
                c8 = slice(c * 8, (c + 1) * 8)
                oc = c - 8
                yb = (7, 3, 0)[c % 3]
                ob = 2
                S.op("pe", lambda: PE.matmul(out=bank(ob), lhsT=CT[:, cs], rhs=st_bf, start=True, stop=True), reads=("CT", "st_bf"), writes=bkeys(ob))
                S.op("dve", lambda: V.tensor_tensor(out=t1.rearrange("p (j q) -> p j q", j=8), in0=bank(ob).rearrange("p (j q) -> p j q", j=8),
                                                    in1=dtv[:, 2, c8].unsqueeze(2).to_broadcast([128, 8, 64]), op=ALU.mult), reads=bkeys(ob) + ("dtv",), writes=("t1",))
                S.op("dve", lambda: V.tensor_tensor(out=t1, in0=t1, in1=bank(yb), op=ALU.add), reads=bkeys(yb) + ("t1",), writes=("t1",))
                S.op("dve", lambda: V.tensor_tensor(out=ygb, in0=t1, in1=z_tm[:, oc, :], op=ALU.mult), reads=("t1", "z_tm"), writes=("ygb",))
                if dbg == "yg" and g == 0:
                    S.dma("sp", dbg_t[oc * 128:(oc + 1) * 128, :], ygb, reads=("ygb",), sem="ddbg")
                S.op("act", lambda: A.activation(out=t2, in_=ygb, func=AF.Square, accum_out=sq[:, g * 8 + oc:g * 8 + oc + 1]), reads=("ygb",), writes=("t2", "sq"))
                pst = bank(4)[:, 128:384].bitcast(BF16)
                for f in range(4):
                    S.op("pe", lambda: PE.transpose(out=pst[:, f * 128:(f + 1) * 128], in_=ygb[:, f * 128:(f + 1) * 128], identity=ident),
                         reads=("ygb", "cbf"), writes=bkeys(4), signal=(f == 3))
                S.op("act", lambda: A.activation(out=ygT_st[:, :, (oc % 4) * 128:(oc % 4 + 1) * 128], in_=pst.rearrange("p (f t) -> p f t", f=4), func=AF.Copy),
                     reads=bkeys(4), writes=("ygT_st",))
                if oc % 4 == 3:
                    c4 = oc // 4
                    S.dma("sp", YGT[g * 512:(g + 1) * 512, c4 * 512:(c4 + 1) * 512].rearrange("(f p) t -> p f t", p=128), ygT_st[:, :, :],
                          reads=("ygT_st",), writes=("YGT",), sem="dsp_y")
                if c < 15:
                    state_update(c)

            sfx = tmp128[:, 0:64].rearrange("p (c j) -> p c j", c=8)
            cdec3 = dtv[:, 4, :].rearrange("p (c j) -> p c j", c=16)
            S.op("dve", lambda: V.memset(sfx[:, 7, :], 1.0), reads=("tmp128",), writes=("tmp128",))
            for c in range(6, -1, -1):
                S.op("dve", lambda: V.tensor_tensor(out=sfx[:, c, :], in0=sfx[:, c + 1, :], in1=cdec3[:, c + 1, :], op=ALU.mult),
                     reads=("tmp128", "dtv"), writes=("tmp128",))
            S.op("dve", lambda: V.tensor_tensor(out=dtv[:, 3, 0:64], in0=dtv[:, 3, 0:64], in1=tmp128[:, 0:64], op=ALU.mult), reads=("tmp128", "dtv"), writes=("dtv",))
            for c in range(8):
                mk_xdtd(c)
                S.op("pe", lambda: PE.matmul(out=bank(1), lhsT=B_tm[:, c, :], rhs=xdtd2[c % 3], start=(c == 0), stop=(c == 7)),
                     reads=("B_tm", "xdtd%d" % (c % 3)), writes=bkeys(1), signal=True)
            S.op("dve", lambda: V.tensor_scalar(out=st, in0=bank(1), scalar1=cst[:, C_FLAG:C_FLAG + 1], scalar2=None, op0=ALU.mult), reads=bkeys(1) + ("cst",), writes=("st",))
            S.op("act", lambda: A.activation(out=st_bf, in_=st, func=AF.Copy), reads=("st",), writes=("st_bf",))
            front_a(8)
            front_b(8)
            front_a(9)
            front_b(9)
            for c in range(8, 16):
                if c + 2 < 16:
                    front_a(c + 2)
                back(c)
                if gjobs:
                    gjobs.pop(0)()
                if c + 2 < 16:
                    front_b(c + 2)

        if dbg == "yg":
            dbg_t = nc.dram_tensor("dbg", [NO, 512], BF16, kind="ExternalOutput").ap()
        gst1 = carve(104480, [128, 512], BF16)
        gate_w = [None]
        gjobs = []

        def mk_gate_job(c256, ct, half):
            def job():
                if ct == 0 and half == 0:
                    gate_w[0] = load_w("w_in", 0, 16, [(GOFF + c256 * 256, 256)])
                wb, kb = gate_w[0]
                col = c256 * 256 + ct * 128
                gi, dtile = col // D, (col % D) // 128
                t0 = (T - NO) + half * 512
                proj_fm(wb, kb, ct, [(wb, kb, 16, 0)], lambda kc, tt: xnT[:, kc, t0:t0 + 512], "xnT", tiles=(0,), b0=1)
                S.op("act", lambda: A.activation(out=gst1, in_=bank(1), func=AF.Identity, bias=cst[:, C_GB + gi * 16 + dtile:C_GB + gi * 16 + dtile + 1]),
                     reads=bkeys(1) + ("cst",), writes=("gst1",))
                S.dma("sp", Gd[gi * D + dtile * 128:gi * D + (dtile + 1) * 128, half * 512:(half + 1) * 512], gst1, reads=("gst1",), writes=("Gd",), sem="dsp_g")
            return job
        for c256 in range(16):
            for ct in range(2):
                for half in range(2):
                    gjobs.append(mk_gate_job(c256, ct, half))
        if stop != "nossd":
            for g in range(8):
                ssd_group(g)
                if dbg == "yg":
                    raise _Stop()

        S.barrier()
        while gjobs:
            gjobs.pop(0)()
        dbg_dump("gates", xnT[:, 0, :], ("xnT", "Gd"), BF16) if False else None

        S.barrier()
        mT = carve(0, [128, 16, 1024], BF16)
        AT_sb = carve(32768, [128, 8, 1024], BF16)
        xres = [carve(32768 + i * 8192, [128, D], F32) for i in range(2)]
        gsb = [carve(49152 + i * 4096, [128, 2, 1024], BF16) for i in range(2)]
        u1 = carve(57344, [128, 1024], F32)
        u2 = [carve(61440 + i * 4096, [128, 1024], F32) for i in range(2)]
        rs_bc = carve(69632, [128, 1024], F32)
        fnw_b = carve(73728, [128, D], F32)
        diag = carve(81920, [128, 128], F32)
        obuf = carve(82432, [128, D], F32)
        YT_sb = xnT[:, :, :].rearrange("p k t -> p (k t)").rearrange("p (c t) -> p c t", c=32)
        Wo_sb = xnT
        S.dma("sp", fnw_b, fnw_d[:, :], writes=("fnw_b",), sem="dconst")
        ssum = stat[:, 32:40]
        S.op("dve", lambda: V.tensor_reduce(out=ssum, in_=sq.rearrange("p (g c) -> p c g", g=8), axis=mybir.AxisListType.X, op=ALU.add), reads=("sq",), writes=("ssum",))
        S.op("dve", lambda: V.tensor_scalar(out=ssum, in0=ssum, scalar1=1.0 / 4096, scalar2=EPS, op0=ALU.mult, op1=ALU.add), reads=("ssum",), writes=("ssum",))
        S.op("act", lambda: A.activation(out=ssum, in_=ssum, func=AF.Sqrt), reads=("ssum",), writes=("ssum",))
        S.op("dve", lambda: V.reciprocal(out=ssum, in_=ssum), reads=("ssum",), writes=("ssum",))
        gq = [0]
        for tt2 in range(1):
            tok = slice(0, NO)
            S.dma("sp", AT_sb, ATd[:, tok].rearrange("(k p) t -> p k t", p=128), reads=("ATd",), writes=("AT_sb", "xres0", "xres1"), sem="dld_a")
            for q4 in range(4):
                S.dma("sp", YT_sb[:, q4 * 8:(q4 + 1) * 8, :], YGT[q4 * 1024:(q4 + 1) * 1024, tok].rearrange("(k p) t -> p k t", p=128),
                      reads=("YGT",), writes=("xnT",), sem="dld_y%d" % q4)
            for tb in range(8):
                S.op("dve", lambda: V.tensor_scalar(out=diag, in0=ident_f, scalar1=ssum[:, tb:tb + 1], scalar2=None, op0=ALU.mult),
                     reads=("cst", "ssum"), writes=("diag",))
                S.op("pe", lambda: PE.matmul(out=bank(4)[:, 0:128], lhsT=ones_f[:, :], rhs=diag, start=True, stop=True), reads=("ones_f", "diag"), writes=bkeys(4))
                S.op("act", lambda: A.activation(out=rs_bc[:, tb * 128:(tb + 1) * 128], in_=bank(4)[:, 0:128], func=AF.Copy), reads=bkeys(4), writes=("rs_bc",))
            for dp in range(8):
                yrhs = lambda kc, tt: YT_sb[:, kc, tt * 512:(tt + 1) * 512]
                wsa, ksa = load_w("w_sb", 0, 16, [(dp * 256, 256)], scale_col=C_NS)
                for ct in range(2):
                    proj_fm(wsa, ksa, ct, [(wsa, ksa, 16, 0)], yrhs, "xnT", b0=2 * ct, first=True, last=False)
                wsb_, ksb = load_w("w_sb", 16, 16, [(dp * 256, 256)], scale_col=C_NS)
                gsl = []
                for ct in range(2):
                    dtile = dp * 2 + ct
                    gs, gk = gsb[gq[0] % 2], "gsb%d" % (gq[0] % 2)
                    gq[0] += 1
                    gsl.append((gs, gk))
                    for gi in range(2):
                        S.dma("sp", gs[:, gi, :], Gd[gi * D + dtile * 128:gi * D + (dtile + 1) * 128, tok], reads=("Gd",), writes=(gk,), sem="dld_" + gk)
                    S.op("act", lambda: A.activation(out=gs[:, :, :], in_=gs[:, :, :], func=AF.Sigmoid), reads=(gk,), writes=(gk,))
                    pf, pk = proj_fm(wsb_, ksb, ct, [(wsb_, ksb, 16, 16)], yrhs, "xnT", b0=2 * ct, first=False, last=True)
                    S.op("dve", lambda: V.tensor_tensor(out=u2[ct], in0=pf, in1=rs_bc, op=ALU.mult), reads=pk + ("rs_bc",), writes=("u2%d" % ct,))
                    S.op("pool", lambda: P.tensor_tensor(out=u2[ct], in0=u2[ct], in1=gs[:, 1, :], op=ALU.mult), reads=("u2%d" % ct, gk), writes=("u2%d" % ct,))
                wab, kab = load_w("w_ab", 0, 8, [(dp * 256, 256)])
                for ct in range(2):
                    dtile = dp * 2 + ct
                    gs, gk = gsl[ct]
                    pf, pk = proj_fm(wab, kab, ct, [(wab, kab, 8, 0)], lambda kc, tt: AT_sb[:, kc, tt * 512:(tt + 1) * 512], "AT_sb")
                    S.op("dve", lambda: V.tensor_tensor(out=u1, in0=pf, in1=gs[:, 0, :], op=ALU.mult), reads=pk + (gk,), writes=("u1",))
                    S.op("pool", lambda: P.tensor_tensor(out=mT[:, dtile, :], in0=u1, in1=u2[ct], op=ALU.add), reads=("u1", "u2%d" % ct), writes=("mT",))
            if dbg == "merged" and tt2 == 0:
                dbg_dump("merged", mT[:, :, :].rearrange("p k t -> p (k t)"), ("mT",), BF16)
            hacc = xnT[:, :, :].rearrange("p k t -> p (k t)").bitcast(F32).rearrange("p (b d) -> p b d", b=8)
            def finalize_rows(tb):
                tg = tb
                xr, kxr = xres[tb % 2], "xres%d" % (tb % 2)
                S.dma("sp", xr, x[(T - NO) + tg * 128:(T - NO) + (tg + 1) * 128, :], reads=(), writes=(kxr, "AT_sb"), sem="dx%d" % (tb % 2))
                S.op("dve", lambda: V.tensor_tensor(out=xr, in0=hacc[:, tb, :], in1=xr, op=ALU.add), reads=("xnT", kxr), writes=(kxr,))
                ss2 = stat[:, 48:49]
                S.op("act", lambda: A.activation(out=obuf, in_=xr, func=AF.Square, accum_out=ss2), reads=(kxr,), writes=("obuf", "ss2"))
                S.op("dve", lambda: V.tensor_scalar(out=ss2, in0=ss2, scalar1=1.0 / D, scalar2=EPS, op0=ALU.mult, op1=ALU.add), reads=("ss2",), writes=("ss2",))
                S.op("act", lambda: A.activation(out=ss2, in_=ss2, func=AF.Sqrt), reads=("ss2",), writes=("ss2",))
                S.op("dve", lambda: V.reciprocal(out=ss2, in_=ss2), reads=("ss2",), writes=("ss2",))
                S.op("dve", lambda: V.scalar_tensor_tensor(out=obuf, in0=xr, scalar=ss2, in1=fnw_b, op0=ALU.mult, op1=ALU.mult),
                     reads=(kxr, "ss2", "fnw_b", "obuf"), writes=("obuf",))
                S.dma("sp", out[tg * 128:(tg + 1) * 128, :], obuf, reads=("obuf",), sem="dout")


            oq = 0
            for c256 in range(8):
                wo, ko = load_w("w_o", 0, 16, [(c256 * 256, 256)])
                for tb in range(8):
                    pb = 4 + (oq % 4)
                    oq += 1
                    for kc in range(KC):
                        S.op("pe", lambda: PE.matmul(out=bank(pb)[:, 0:256], lhsT=mT[:, kc, tb * 128:(tb + 1) * 128], rhs=wo[:, kc, 0:256],
                                                     start=(kc == 0), stop=(kc == KC - 1)),
                             reads=("mT", ko), writes=bkeys(pb), signal=(kc == KC - 1))
                    S.op("act", lambda: A.activation(out=hacc[:, tb, c256 * 256:(c256 + 1) * 256], in_=bank(pb)[:, 0:256], func=AF.Copy),
                         reads=bkeys(pb), writes=("xnT",))
                    if c256 == 7:
                        finalize_rows(tb)
    try:
        dbg_dump("xnT", xnT[:, :, :].rearrange("p k t -> p (k t)"), ("xnT",), BF16)
        if stop != "P0":
            phases()
    except _Stop:
        pass
    S.finish()
    build.last_wrec = wrec
    return nc


def host_inputs(inputs, core):
    b, h = core // 2, core % 2
    f32 = np.float32
    m = {}
    xc = np.zeros((T, D), dtype=f32)
    pc = np.zeros((T,), dtype=np.int32)
    if h == 1:
        xc[:] = inputs["x"][b]
        pc[:] = inputs["positions"][b]
    else:
        xc[T - NO:] = inputs["x"][b, 0:NO]
        pc[T - NO:] = inputs["positions"][b, 0:NO]
    m["x"] = xc
    m["pos"] = np.ascontiguousarray(np.broadcast_to(pc[None, :], (32, T)), dtype=np.int32)
    m["w_in"] = np.ascontiguousarray(inputs["w_in"][0], dtype=f32)
    m["w_ab"] = np.ascontiguousarray(inputs["w_attn_br"][0], dtype=f32)
    m["w_sb"] = np.ascontiguousarray(inputs["w_ssd_br"][0], dtype=f32)
    m["w_o"] = np.ascontiguousarray(inputs["w_out"][0], dtype=f32)
    m["fnw"] = np.ascontiguousarray(np.broadcast_to(inputs["final_norm_w"][None, :], (128, D)), dtype=f32)
    c = np.zeros((128, 1280), dtype=f32)
    idx = np.arange(128)
    c[:, 0:128] = np.eye(128, dtype=f32)
    c[:, 128:256] = (idx[:, None] <= idx[None, :]).astype(f32)
    c[:, 256:384] = (idx[:, None] > idx[None, :]).astype(f32)
    R = np.zeros((32, 32), dtype=f32)
    for i in range(16):
        R[i + 16, i] = -1.0
        R[i, i + 16] = 1.0
    c[0:32, 384:416] = R
    c[:, 416:544] = np.where(idx[None, :] >= idx[:, None], 0.0, NEG)
    c[:, 544:672] = np.where(idx[:, None] >= idx[None, :], 0.0, NEG)
    c[:, 672:688] = inputs["norm_w"][0].reshape(KC, 128).T
    half = 16
    invf = (500000.0 ** (-np.arange(half, dtype=np.float64) / half)).astype(f32)
    c[0:32, 688] = np.concatenate([invf, invf])
    c[:, 689:721] = inputs["gate_b"][0].reshape(2 * 16, 128).T
    cw = inputs["conv_w"][0]
    c[:, 721:913] = cw.T.reshape(48, 128, 4).transpose(1, 0, 2).reshape(128, 192)
    c[:, 913:961] = inputs["conv_b"][0].reshape(48, 128).T
    c[:, 961:993] = inputs["ssd_norm_w"][0].reshape(32, 128).T
    c[:, 1024:1088] = inputs["dt_bias"][0][None, :]
    c[:, 1088:1152] = inputs["a_log"][0][None, :]
    c[:, 1152:1216] = inputs["d_skip"][0][None, :]
    c[:, 1216] = 0.0 if h == 1 else NEG
    c[0:64, 1217] = 0.0 if h == 1 else NEG
    c[:, 1218] = 1.0 if h == 1 else 0.0
    m["cst"] = c
    return m


def kernel(**inputs):
    inputs = {k: np.asarray(v) for k, v in inputs.items()}
    build()
    nc = build(wplan=list(build.last_wrec))
    in_maps = [host_inputs(inputs, c) for c in range(8)]
    res = run_bass_kernel_spmd(nc, in_maps, core_ids=list(range(8)))
    outp = np.empty((4, 2048, D), dtype=np.float32)
    for c in range(8):
        outp[c // 2, (c % 2) * NO:(c % 2 + 1) * NO] = res.results[c]["out"]
    return outp
```
